# Optimizing a Trainium2 kernel written in Bass

```python
import jax, jax.numpy as jnp
from jax import lax
import numpy as np

D_MODEL = 1024
BATCH = 2
SEQ = 8192
DEPTH = 1

GRID_W = 64
D_MIX = 2 * D_MODEL
SSM_HEADS = 16
SSM_HEAD_DIM = 64
D_SSM = SSM_HEADS * SSM_HEAD_DIM
SSM_GROUPS = 2
HEADS_PER_GROUP = SSM_HEADS // SSM_GROUPS
D_STATE = 128
D_CONV = 5
CHUNK = 128
CONV_DIM = D_SSM + 2 * SSM_GROUPS * D_STATE
NA_HEADS = 16
NA_HEAD_DIM = 64
D_NA = NA_HEADS * NA_HEAD_DIM
NA_MAX_ROWS = 8
NA_COLS = 16
D_FF = 4 * D_MODEL
PROJ_DIM = D_SSM + CONV_DIM + 2 * SSM_HEADS + 3 * D_NA
EPS = 1e-5

kernel_name = "hymba_ssd_natten_sqrelu_block"


def rmsnorm(x, w):
    xf = x.astype(jnp.float32)
    y = xf * lax.rsqrt(jnp.mean(xf * xf, axis=-1, keepdims=True) + EPS)
    return (y * w.astype(jnp.float32)).astype(x.dtype)


def depthwise_conv_centred(u, w, b):
    pad_l = (D_CONV - 1) // 2
    out = lax.conv_general_dilated(
        u, w.astype(u.dtype)[:, None, :], window_strides=(1,),
        padding=[(pad_l, D_CONV - 1 - pad_l)],
        dimension_numbers=("NWC", "WIO", "NWC"),
        feature_group_count=u.shape[-1])
    return out + b.astype(u.dtype)


def ssd_chunked(xdt, log_a, b_mat, c_mat):
    bsz, t = xdt.shape[:2]
    nc = t // CHUNK
    xc = xdt.astype(jnp.float32).reshape(bsz, nc, CHUNK, SSM_GROUPS, HEADS_PER_GROUP, SSM_HEAD_DIM)
    ac = log_a.astype(jnp.float32).reshape(bsz, nc, CHUNK, SSM_GROUPS, HEADS_PER_GROUP)
    bc = b_mat.astype(jnp.float32).reshape(bsz, nc, CHUNK, SSM_GROUPS, D_STATE)
    cc = c_mat.astype(jnp.float32).reshape(bsz, nc, CHUNK, SSM_GROUPS, D_STATE)
    a_cum = jnp.cumsum(ac, axis=2)
    seg = a_cum[:, :, :, None] - a_cum[:, :, None, :]
    lower = jnp.tril(jnp.ones((CHUNK, CHUNK), dtype=bool))[None, None, :, :, None, None]
    decay = jnp.exp(jnp.where(lower, seg, -jnp.inf))
    cb = jnp.einsum("bclgn,bcsgn->bclsg", cc, bc)
    y_diag = jnp.einsum("bclsge,bcsgep->bclgep", cb[..., None] * decay, xc)
    decay_to_end = jnp.exp(a_cum[:, :, -1:] - a_cum)
    chunk_states = jnp.einsum("bclgn,bclgep->bcgepn", bc, xc * decay_to_end[..., None])
    chunk_decay = jnp.exp(a_cum[:, :, -1])

    def step(h, inp):
        s_c, d_c = inp
        return d_c[..., None, None] * h + s_c, h

    h0 = jnp.zeros((bsz, SSM_GROUPS, HEADS_PER_GROUP, SSM_HEAD_DIM, D_STATE), jnp.float32)
    _, h_in = lax.scan(step, h0, (jnp.moveaxis(chunk_states, 1, 0), jnp.moveaxis(chunk_decay, 1, 0)))
    h_in = jnp.moveaxis(h_in, 0, 1)
    y_off = jnp.einsum("bclgn,bcgepn->bclgep", cc, h_in) * jnp.exp(a_cum)[..., None]
    return (y_diag + y_off).reshape(bsz, t, SSM_GROUPS, HEADS_PER_GROUP, SSM_HEAD_DIM)


def ssd_mixer(z, xbc, dt_raw, conv_w, conv_b, dt_bias_f, dt_bias_b, a_log_f, a_log_b, d_skip, norm_w):
    bsz, t, _ = xbc.shape
    xbc = jax.nn.silu(depthwise_conv_centred(xbc, conv_w, conv_b))
    xs, b_mat, c_mat = jnp.split(xbc, [D_SSM, D_SSM + SSM_GROUPS * D_STATE], axis=-1)
    xs = xs.astype(jnp.float32).reshape(bsz, t, SSM_GROUPS, HEADS_PER_GROUP, SSM_HEAD_DIM)
    b_mat = b_mat.reshape(bsz, t, SSM_GROUPS, D_STATE)
    c_mat = c_mat.reshape(bsz, t, SSM_GROUPS, D_STATE)
    dt_f_raw, dt_b_raw = jnp.split(dt_raw.astype(jnp.float32), 2, axis=-1)
    dt_f = jax.nn.softplus(dt_f_raw + dt_bias_f.astype(jnp.float32)).reshape(bsz, t, SSM_GROUPS, HEADS_PER_GROUP)
    dt_b = jax.nn.softplus(dt_b_raw + dt_bias_b.astype(jnp.float32)).reshape(bsz, t, SSM_GROUPS, HEADS_PER_GROUP)
    a_f = -jnp.exp(a_log_f.astype(jnp.float32)).reshape(SSM_GROUPS, HEADS_PER_GROUP)
    a_b = -jnp.exp(a_log_b.astype(jnp.float32)).reshape(SSM_GROUPS, HEADS_PER_GROUP)
    flip = lambda u: jnp.flip(u, axis=1)
    y_f = ssd_chunked(xs * dt_f[..., None], dt_f * a_f, b_mat, c_mat)
    y_b = flip(ssd_chunked(flip(xs * dt_b[..., None]), flip(dt_b * a_b), flip(b_mat), flip(c_mat)))
    y = y_f + y_b + d_skip.astype(jnp.float32).reshape(SSM_GROUPS, HEADS_PER_GROUP)[..., None] * xs
    gw = D_SSM // SSM_GROUPS
    g = y.reshape(bsz, t, SSM_GROUPS, gw) * jax.nn.silu(z.astype(jnp.float32)).reshape(bsz, t, SSM_GROUPS, gw)
    g = g * lax.rsqrt(jnp.mean(g * g, axis=-1, keepdims=True) + EPS)
    g = g * norm_w.astype(jnp.float32).reshape(SSM_GROUPS, gw)
    return g.reshape(bsz, t, D_SSM).astype(z.dtype)


def neighbourhood_attention(q, k, v, q_norm_w, k_norm_w, rpb):
    bsz, t, _ = q.shape
    rows = t // GRID_W
    kh = min(NA_MAX_ROWS, rows)
    q = rmsnorm(q.reshape(bsz, t, NA_HEADS, NA_HEAD_DIM), q_norm_w) * (NA_HEAD_DIM ** -0.5)
    k = rmsnorm(k.reshape(bsz, t, NA_HEADS, NA_HEAD_DIM), k_norm_w)
    qg = q.reshape(bsz, rows, GRID_W, NA_HEADS, NA_HEAD_DIM)
    kg = k.reshape(bsz, rows, GRID_W, NA_HEADS, NA_HEAD_DIM)
    vg = v.reshape(bsz, rows, GRID_W, NA_HEADS, NA_HEAD_DIM)
    col = jnp.arange(GRID_W)
    c0 = jnp.clip(col - NA_COLS // 2, 0, GRID_W - NA_COLS)
    col_win = c0[:, None] + jnp.arange(NA_COLS)
    col_rel = col_win - col[:, None] + (NA_COLS - 1)

    def row_block(i):
        r0 = jnp.clip(i - kh // 2, 0, rows - kh)
        q_i = lax.dynamic_index_in_dim(qg, i, axis=1, keepdims=False)
        k_rows = lax.dynamic_slice_in_dim(kg, r0, kh, axis=1)
        v_rows = lax.dynamic_slice_in_dim(vg, r0, kh, axis=1)
        k_win = k_rows[:, :, col_win]
        v_win = v_rows[:, :, col_win]
        row_rel = r0 + jnp.arange(kh) - i + (NA_MAX_ROWS - 1)
        bias = rpb[:, row_rel[None, :, None], col_rel[:, None, :]]
        s = jnp.einsum("bjhd,bajchd->bhjac", q_i, k_win).astype(jnp.float32) + bias.astype(jnp.float32)[None]
        p = jax.nn.softmax(s.reshape(bsz, NA_HEADS, GRID_W, kh * NA_COLS), axis=-1)
        p = p.reshape(bsz, NA_HEADS, GRID_W, kh, NA_COLS).astype(v.dtype)
        return jnp.einsum("bhjac,bajchd->bjhd", p, v_win)

    out = lax.map(row_block, jnp.arange(rows))
    return jnp.moveaxis(out, 0, 1).reshape(bsz, t, D_NA)


def hybrid_layer(x, norm_mix_w, w_in, conv_w, conv_b, dt_bias_fwd, dt_bias_bwd, a_log_fwd, a_log_bwd,
                 d_skip, ssm_norm_w, q_norm_w, k_norm_w, rel_pos_bias, w_out, norm_mlp_w, w_mlp_in, w_mlp_out):
    h = rmsnorm(x, norm_mix_w)
    proj = h @ w_in
    offs = np.cumsum([D_SSM, CONV_DIM, 2 * SSM_HEADS, D_NA, D_NA]).tolist()
    z, xbc, dt_raw, q, k, v = jnp.split(proj, offs, axis=-1)
    y_ssm = ssd_mixer(z, xbc, dt_raw, conv_w, conv_b, dt_bias_fwd, dt_bias_bwd,
                      a_log_fwd, a_log_bwd, d_skip, ssm_norm_w)
    y_na = neighbourhood_attention(q, k, v, q_norm_w, k_norm_w, rel_pos_bias)
    x = x + jnp.concatenate([y_ssm, y_na.astype(y_ssm.dtype)], axis=-1) @ w_out
    h = rmsnorm(x, norm_mlp_w)
    return x + jnp.square(jax.nn.relu(h @ w_mlp_in)) @ w_mlp_out


def setup_inputs(seed: int = 0) -> dict:
    key = jax.random.key(seed)
    ks = jax.random.split(key, 20)
    f32 = jnp.float32
    L = DEPTH
    nrm = lambda k, shape, s: jax.random.normal(k, shape, f32) * s
    dt0 = jnp.exp(jax.random.uniform(ks[4], (L, 2, SSM_HEADS), f32, np.log(1e-3), np.log(1e-1)))
    dt_bias = dt0 + jnp.log(-jnp.expm1(-dt0))
    a_log = jnp.log(jax.random.uniform(ks[5], (L, 2, SSM_HEADS), f32, 1.0, 16.0))
    return {
        "x": jax.random.normal(ks[0], (BATCH, SEQ, D_MODEL), f32),
        "norm_mix_w": 1.0 + nrm(ks[1], (L, D_MODEL), 0.02),
        "w_in": nrm(ks[2], (L, D_MODEL, PROJ_DIM), D_MODEL ** -0.5),
        "conv_w": nrm(ks[3], (L, D_CONV, CONV_DIM), D_CONV ** -0.5),
        "conv_b": nrm(ks[6], (L, CONV_DIM), 0.02),
        "dt_bias_fwd": dt_bias[:, 0],
        "dt_bias_bwd": dt_bias[:, 1],
        "a_log_fwd": a_log[:, 0],
        "a_log_bwd": a_log[:, 1],
        "d_skip": 1.0 + nrm(ks[7], (L, SSM_HEADS), 0.02),
        "ssm_norm_w": 1.0 + nrm(ks[8], (L, D_SSM), 0.02),
        "q_norm_w": 1.0 + nrm(ks[9], (L, NA_HEAD_DIM), 0.02),
        "k_norm_w": 1.0 + nrm(ks[10], (L, NA_HEAD_DIM), 0.02),
        "rel_pos_bias": nrm(ks[11], (L, NA_HEADS, 2 * NA_MAX_ROWS - 1, 2 * NA_COLS - 1), 0.1),
        "w_out": nrm(ks[12], (L, D_MIX, D_MODEL), D_MIX ** -0.5),
        "norm_mlp_w": 1.0 + nrm(ks[13], (L, D_MODEL), 0.02),
        "w_mlp_in": nrm(ks[14], (L, D_MODEL, D_FF), D_MODEL ** -0.5),
        "w_mlp_out": nrm(ks[15], (L, D_FF, D_MODEL), D_FF ** -0.5),
    }


def reference(x, norm_mix_w, w_in, conv_w, conv_b, dt_bias_fwd, dt_bias_bwd, a_log_fwd, a_log_bwd,
              d_skip, ssm_norm_w, q_norm_w, k_norm_w, rel_pos_bias, w_out, norm_mlp_w, w_mlp_in, w_mlp_out):
    for layer in range(DEPTH):
        x = hybrid_layer(x, norm_mix_w[layer], w_in[layer], conv_w[layer], conv_b[layer],
                         dt_bias_fwd[layer], dt_bias_bwd[layer], a_log_fwd[layer], a_log_bwd[layer],
                         d_skip[layer], ssm_norm_w[layer], q_norm_w[layer], k_norm_w[layer],
                         rel_pos_bias[layer], w_out[layer], norm_mlp_w[layer],
                         w_mlp_in[layer], w_mlp_out[layer])
    return x
```

```python
import numpy as np
from contextlib import ExitStack
import concourse.bass as bass
import concourse.mybir as mybir
from concourse.bass_utils import run_bass_kernel_spmd

F32 = mybir.dt.float32
BF16 = mybir.dt.bfloat16
AF = mybir.ActivationFunctionType
ALU = mybir.AluOpType
AX = mybir.AxisListType

EPS = 1e-5
import os
CUT = int(os.environ.get('CUT', '0'))
SKIP1 = int(os.environ.get('SKIP1', '0'))
NAJ = int(os.environ.get('NAJ', '8'))
NAM = int(os.environ.get('NAM', '16'))
NAQ = int(os.environ.get('NAQ', '0'))
NAP = int(os.environ.get('NAP', '9'))
NAX = int(os.environ.get('NAX', '0'))
NEG = -30000.0
NSLOT = 20
NTOK = NSLOT * 128
OWN0 = 2


class Sched:
    ENG = ("pe", "act", "dve", "pool", "sp")

    def __init__(self, nc, stack, dma_sems=()):
        self.nc = nc
        self.sem = {}
        for e in self.ENG:
            self.sem[e] = stack.enter_context(nc.semaphore("s_" + e))
        for d in dma_sems:
            self.sem[d] = stack.enter_context(nc.semaphore("d_" + d))
        self.cnt = {k: 0 for k in self.sem}
        self.known = {e: {} for e in self.ENG}
        self.lastw = {}
        self.readers = {}
        self.prog = {e: [] for e in self.ENG}

    def _deps(self, e, reads, writes):
        need = {}

        def add(s, v):
            if s == e and v > self.cnt[e]:
                return
            if need.get(s, 0) < v:
                need[s] = v

        for k in reads:
            w = self.lastw.get(k)
            if w:
                add(*w)
        for k in writes:
            w = self.lastw.get(k)
            if w:
                add(*w)
            for s, v in self.readers.get(k, {}).items():
                add(s, v)
        waits = []
        for s, v in need.items():
            if self.known[e].get(s, 0) < v:
                self.known[e][s] = v
                waits.append((s, v))
        return waits

    def _mark(self, s, val, reads, writes):
        for k in reads:
            self.readers.setdefault(k, {})[s] = val
        for k in writes:
            self.lastw[k] = (s, val)
            self.readers[k] = {}

    def op(self, e, fn, reads=(), writes=(), signal=True):
        waits = self._deps(e, reads, writes)
        if signal:
            self.cnt[e] += 1
            val = self.cnt[e]
        else:
            val = self.cnt[e] + 1
        self._mark(e, val, reads, writes)
        self.prog[e].append((waits, fn, (e, 1) if signal else None))

    def dma(self, q, out, in_, dsem, reads=(), writes=()):
        waits = self._deps(q, reads, writes)
        self.cnt[dsem] += 16
        self._mark(dsem, self.cnt[dsem], reads, writes)
        self.prog[q].append((waits, lambda eng, o=out, i=in_: eng.dma_start(out=o, in_=i), (dsem, 16)))

    def custom(self, q, fn, dsem, inc, reads=(), writes=()):
        waits = self._deps(q, reads, writes)
        self.cnt[dsem] += inc
        self._mark(dsem, self.cnt[dsem], reads, writes)
        self.prog[q].append((waits, fn, (dsem, inc)))

    def barrier(self):
        for e in self.ENG:
            waits = []
            for s, v in self.cnt.items():
                if s == e or v == 0:
                    continue
                if self.known[e].get(s, 0) < v:
                    self.known[e][s] = v
                    waits.append((s, v))
            if waits:
                self.prog[e].append((waits, None, None))
        self.lastw = {}
        self.readers = {}

    def emit(self, block):
        eng_obj = {"pe": "tensor", "act": "scalar", "dve": "vector", "pool": "gpsimd", "sp": "sync"}
        for e in self.ENG:
            prog = self.prog[e]
            sems = self.sem

            def body(eng, prog=prog, sems=sems):
                for waits, fn, inc in prog:
                    for s, v in waits:
                        eng.wait_ge(sems[s], v)
                    if fn is not None:
                        ins = fn(eng)
                        if inc is not None:
                            ins.then_inc(sems[inc[0]], inc[1])

            getattr(block, eng_obj[e])(body)


def bc3(ap2, n):
    return ap2.unsqueeze(2).to_broadcast([ap2.shape[0], ap2.shape[1], n])


def build_program(debug=False, stop=99):
    nc = bass.Bass("TRN2", target_bir_lowering=False)
    din = lambda name, shape: nc.dram_tensor(name, shape, F32, kind="ExternalInput").ap()
    xs_d = din("xs", [NTOK, 1024])
    w_in_d = din("w_in", [1024, 5664])
    w_out_d = din("w_out", [2048, 1024])
    w1_d = din("w1", [1024, 4096])
    w2_d = din("w2", [4096, 1024])
    colp_d = din("colp", [128, 8 * 3 + 12 * 5 + 12 + 2])
    rowp_d = din("rowp", [128, 32 + 32 + 16 + 16])
    natab_d = din("natab", [16, 128, 3200])
    out_d = nc.dram_tensor("out", [2048, 1024], F32, kind="ExternalOutput").ap()
    if debug:
        dbg_d = nc.dram_tensor("dbg", [128, 16 * 2048 + 16384], F32, kind="ExternalOutput").ap()
    send_d = nc.dram_tensor("send_b", [128, 2080], F32)
    recv_d = nc.dram_tensor("recv_b", [1024, 2080], F32)
    w_in_v = w_in_d.rearrange("(k p) c -> p k c", p=128)

    with ExitStack() as top, nc.allow_low_precision("bf16 matmul operands, fp32 accumulation"):
        dsems = ["ldx0", "ldx1", "cst", "wA0", "wA1", "wB", "snd", "cc", "rcv0", "rcv1", "rcvt",
                 "tab0", "tab1", "wq0", "wq1", "wo0", "wo1", "w10", "w11", "w20", "w21", "xr0", "xr1",
                 "st0", "st1", "dbg"]
        S = Sched(nc, top, dma_sems=dsems)
        sbt = lambda st, name, shape, dt: st.enter_context(nc.sbuf_tensor("sb_" + name, shape, dt))
        ps = top.enter_context(nc.psum_tensor("ps", [128, 4096], F32))
        psb_all = ps[:, :].bitcast(BF16)

        def PS(b, lo=0, hi=512):
            return ps[:, b * 512 + lo: b * 512 + hi]

        def PSB(b, lo=0, hi=1024):
            return psb_all[:, b * 1024 + lo: b * 1024 + hi]

        def pk(b):
            return "ps%d" % b

        mixS = sbt(top, "mixS", [128, 8, 2048], BF16)
        colp = sbt(top, "colp", [128, 98], F32)
        rowp = sbt(top, "rowp", [128, 96], F32)
        ident = sbt(top, "ident", [128, 128], BF16)
        cf = sbt(top, "cf", [128, 5, 128], F32)
        ones = sbt(top, "ones", [128, 128], F32)
        nw1 = colp[:, 0:8]; nw2 = colp[:, 8:16]; ssmw = colp[:, 16:24]
        convw = colp[:, 24:84].rearrange("p (c j) -> p c j", j=5); convb = colp[:, 84:96]
        qkw = colp[:, 96:98]
        dtb = rowp[:, 0:32]; alog = rowp[:, 32:64]; dsk = rowp[:, 64:80]; cmask = rowp[:, 80:96]
        identf = cf[:, 0, :]; Um = cf[:, 1, :]; Lm = cf[:, 2, :]; SG = cf[:, 3, :]; SL = cf[:, 4, :]

        blk = top.enter_context(nc.Block())

        S.dma("sp", colp[:], colp_d[:, :], "cst", writes=["colp"])
        S.dma("sp", rowp[:], rowp_d[:, :], "cst", writes=["rowp"])
        S.op("pool", lambda e: e.memset(cf[:], 1.0), writes=["cf"])
        S.op("pool", lambda e: e.memset(ones[:], 1.0), writes=["ones"])
        sel = [(0, [[-1, 128]], ALU.is_equal, 1), (1, [[1, 128]], ALU.is_ge, -1), (2, [[-1, 128]], ALU.is_ge, 1),
               (3, [[-1, 128]], ALU.is_gt, 1), (4, [[1, 128]], ALU.is_gt, -1)]
        for i, pat, cmp_, cm in sel:
            S.op("pool", lambda e, i=i, pat=pat, cmp_=cmp_, cm=cm: e.affine_select(
                out=cf[:, i, :], in_=cf[:, i, :], pattern=pat, compare_op=cmp_, fill=0.0, base=0,
                channel_multiplier=cm), reads=["cf"], writes=["cf"])
        S.op("dve", lambda e: e.tensor_copy(out=ident[:], in_=identf), reads=["cf"], writes=["ident"])

        def phase_norm(st, hT, sfx):
            xt = [sbt(st, "xt%d%s" % (i, sfx), [128, 1024], F32) for i in range(2)]
            hn = [sbt(st, "hn%d%s" % (i, sfx), [128, 1024], BF16) for i in range(2)]
            junk = sbt(st, "junkA" + sfx, [128, 1024], BF16)
            ss = sbt(st, "ssA" + sfx, [128, NSLOT], F32)
            for u in range(NSLOT):
                b = u % 2
                S.dma("sp", xt[b][:], xs_d[u * 128:(u + 1) * 128, :], "ldx%d" % b, writes=["xt%d" % b])
                S.op("act", lambda e, b=b, u=u: e.activation(out=junk[:], in_=xt[b][:], func=AF.Square,
                                                           accum_out=ss[:, u:u + 1]),
                     reads=["xt%d" % b], writes=["junkA", "ssA%d" % u])
                S.op("act", lambda e, u=u: e.activation(out=ss[:, u:u + 1], in_=ss[:, u:u + 1], func=AF.Sqrt,
                                                      bias=EPS, scale=1.0 / 1024),
                     reads=["ssA%d" % u], writes=["ssA%d" % u])
                S.op("dve", lambda e, u=u: e.reciprocal(out=ss[:, u:u + 1], in_=ss[:, u:u + 1]),
                     reads=["ssA%d" % u], writes=["ssA%d" % u])
                S.op("dve", lambda e, b=b, u=u: e.tensor_scalar(out=hn[b][:], in0=xt[b][:], scalar1=ss[:, u:u + 1],
                                                              scalar2=None, op0=ALU.mult),
                     reads=["xt%d" % b, "ssA%d" % u], writes=["hn%d" % b])
                for k in range(8):
                    S.op("pe", lambda e, b=b, k=k: e.transpose(out=PSB(b, k * 128, (k + 1) * 128),
                                                             in_=hn[b][:, k * 128:(k + 1) * 128], identity=ident[:]),
                         reads=["hn%d" % b, "ident"], writes=[pk(b)], signal=(k == 7))
                S.op("dve", lambda e, b=b, u=u: e.tensor_tensor(
                    out=hT[:, :, u * 128:(u + 1) * 128], in0=PSB(b).rearrange("p (k t) -> p k t", k=8),
                    in1=bc3(nw1, 128), op=ALU.mult), reads=[pk(b), "colp"], writes=["hT"])

        with ExitStack() as s1:
          if not SKIP1:
              xs_tok = sbt(s1, "xs_tok", [128, 16, 1024], BF16)
              B_tok = sbt(s1, "B_tok", [128, 16, 256], BF16)
              BT = sbt(s1, "BT", [128, 2, 2048], BF16)
              CT = sbt(s1, "CT", [128, 2, 2048], BF16)
              zs = sbt(s1, "zs", [128, 16, 1024], BF16)
              dtv = sbt(s1, "dtv", [128, 512], F32)
              av = sbt(s1, "av", [128, 512], F32)
              wdtv = sbt(s1, "wdtv", [128, 512], F32)
              eac = sbt(s1, "eac", [128, 512], F32)
              etot = sbt(s1, "etot", [128, 512], F32)
              totv = sbt(s1, "totv", [128, 512], F32)
              with ExitStack() as s1a:
                  hT = sbt(s1a, "hT", [128, 8, NTOK], BF16)
                  with ExitStack() as s1n:
                      phase_norm(s1n, hT, "a")
                      S.barrier()
                      if debug:
                          S.dma("pool", dbg_d[:, 32768 + 10240:32768 + 11264], hT[:, 0, 0:1024], "dbg", reads=["hT"])
                          S.barrier()
                      if stop == 1:
                          S.barrier()
                          S.emit(blk)
                          return nc
                  with ExitStack() as sz:
                      wz = sbt(sz, "wz", [128, 8, 1024], BF16)
                      S.dma("pool", wz[:, :, 0:512], w_in_v[:, :, 0:512], "wB", writes=["wz"])
                      S.dma("pool", wz[:, :, 512:1024], w_in_v[:, :, 512:1024], "wB", writes=["wz"])
                      for c in range(16):
                          for half in range(2):
                              for k in range(8):
                                  S.op("pe", lambda e, c=c, half=half, k=k: e.matmul(
                                      PS(6 + half), lhsT=hT[:, k, (c + OWN0) * 128:(c + OWN0 + 1) * 128],
                                      rhs=wz[:, k, half * 512:(half + 1) * 512], start=(k == 0), stop=(k == 7)),
                                       reads=["hT", "wz"], writes=[pk(6 + half)], signal=(k == 7))
                              S.op("act", lambda e, c=c, half=half: e.activation(out=zs[:, c, half * 512:(half + 1) * 512],
                                                                               in_=PS(6 + half), func=AF.Silu),
                                   reads=[pk(6 + half)], writes=["zs"])
                      if debug:
                          S.dma("pool", dbg_d[:, 32768 + 11264:32768 + 11776], wz[:, 0, 0:512], "dbg", reads=["wz"])
                          S.dma("pool", dbg_d[:, 32768 + 11776:32768 + 12800], zs[:, 0, :], "dbg", reads=["zs"])
                      S.barrier()
                      if stop == 2:
                          S.barrier()
                          S.emit(blk)
                          return nc
                  wch = [sbt(s1a, "wch%d" % i, [128, 8, 128], BF16) for i in range(2)]
                  raw = [sbt(s1a, "raw0", [128, 2052], F32)] * 2
                  acc = [sbt(s1a, "acc0", [128, 2048], F32)] * 2
                  xsT = [sbt(s1a, "xsT0", [128, 2048], BF16)] * 2
                  wdt = sbt(s1a, "wdt", [128, 8, 32], BF16)
                  dtx = sbt(s1a, "dtx", [128, 512], F32)
                  cum = sbt(s1a, "cum", [128, 512], F32)
                  aexp = sbt(s1a, "aexp", [128, 32], F32)
                  S.dma("pool", wch[0][:], w_in_v[:, :, 1024:1152], "wA0", writes=["wch0"])
                  S.dma("pool", wdt[:], w_in_v[:, :, 2560:2592], "wB", writes=["wdt"])

                  for cc in range(12):
                      wb = wch[cc % 2]
                      wkey = "wch%d" % (cc % 2)
                      rb = 0
                      if cc + 1 < 12:
                          S.dma("pool", wch[(cc + 1) % 2][:], w_in_v[:, :, 1024 + (cc + 1) * 128:1024 + (cc + 2) * 128],
                                "wA%d" % ((cc + 1) % 2), writes=["wch%d" % ((cc + 1) % 2)])
                      for w in range(6):
                          bank = 2 + (w % 2)
                          t0 = 254 + 342 * w
                          for k in range(8):
                              S.op("pe", lambda e, bank=bank, wb=wb, k=k, cc=cc, t0=t0: e.matmul(
                                  PS(bank, 0, 342), lhsT=wb[:, k, :],
                                  rhs=hT[:, k, t0:t0 + 342], start=(k == 0), stop=(k == 7)),
                                   reads=[wkey, "hT"], writes=[pk(bank)], signal=(k == 7))
                          S.op("act", lambda e, bank=bank, rb=rb, w=w: e.copy(out=raw[rb][:, 342 * w:342 * (w + 1)],
                                                                            in_=PS(bank, 0, 342)),
                               reads=[pk(bank)], writes=["raw%d" % rb])
                      ce = "dve"
                      S.op(ce, lambda e, rb=rb, cc=cc: e.tensor_scalar(out=acc[rb][:], in0=raw[rb][:, 0:2048],
                                                                     scalar1=convw[:, cc, 0:1], scalar2=None,
                                                                     op0=ALU.mult),
                           reads=["raw%d" % rb, "colp"], writes=["acc%d" % rb])
                      for j in range(1, 5):
                          S.op(ce, lambda e, rb=rb, cc=cc, j=j: e.scalar_tensor_tensor(
                              out=acc[rb][:], in0=raw[rb][:, j:j + 2048], scalar=convw[:, cc, j:j + 1], in1=acc[rb][:],
                              op0=ALU.mult, op1=ALU.add), reads=["raw%d" % rb, "acc%d" % rb], writes=["acc%d" % rb])
                      if cc < 8:
                          dst = xsT[rb][:]; dkey = "xsT%d" % rb
                      elif cc < 10:
                          dst = BT[:, cc - 8, :]; dkey = "BT"
                      else:
                          dst = CT[:, cc - 10, :]; dkey = "CT"
                      S.op("act", lambda e, dst=dst, rb=rb, cc=cc: e.activation(out=dst, in_=acc[rb][:], func=AF.Silu,
                                                                              bias=convb[:, cc:cc + 1]),
                           reads=["acc%d" % rb, "colp"], writes=[dkey])
                      if cc < 10:
                          for hb in range(2):
                              bank = hb
                              for t in range(8):
                                  tt = hb * 8 + t
                                  S.op("pe", lambda e, bank=bank, t=t, tt=tt, dst=dst: e.transpose(
                                      out=PSB(bank, t * 128, (t + 1) * 128), in_=dst[:, tt * 128:(tt + 1) * 128],
                                      identity=ident[:]), reads=[dkey, "ident"], writes=[pk(bank)], signal=(t == 7))
                              if cc < 8:
                                  o = xs_tok[:, hb * 8:(hb + 1) * 8, cc * 128:(cc + 1) * 128]; okey = "xs_tok"
                              else:
                                  o = B_tok[:, hb * 8:(hb + 1) * 8, (cc - 8) * 128:(cc - 7) * 128]; okey = "B_tok"
                              ee = "act" if hb == 0 else "dve"
                              if ee == "act":
                                  S.op("act", lambda e, o=o, bank=bank: e.copy(
                                      out=o, in_=PSB(bank).rearrange("p (t f) -> p t f", t=8)),
                                       reads=[pk(bank)], writes=[okey])
                              else:
                                  S.op("dve", lambda e, o=o, bank=bank: e.tensor_copy(
                                      out=o, in_=PSB(bank).rearrange("p (t f) -> p t f", t=8)),
                                       reads=[pk(bank)], writes=[okey])

                  for c in range(16):
                      for k in range(8):
                          S.op("pe", lambda e, c=c, k=k: e.matmul(PS(4, c * 32, (c + 1) * 32),
                                                                lhsT=hT[:, k, (c + OWN0) * 128:(c + OWN0 + 1) * 128],
                                                                rhs=wdt[:, k, :], start=(k == 0), stop=(k == 7)),
                               reads=["hT", "wdt"], writes=[pk(4)], signal=(k == 7 and c == 15))
                  S.op("dve", lambda e: e.tensor_tensor(out=dtx[:].rearrange("p (c j) -> p c j", c=16),
                                                        in0=PS(4).rearrange("p (c j) -> p c j", c=16),
                                                        in1=dtb.unsqueeze(1).to_broadcast([128, 16, 32]), op=ALU.add),
                       reads=[pk(4), "rowp"], writes=["dtx"])
                  S.op("act", lambda e: e.activation(out=dtx[:], in_=dtx[:], func=AF.Exp), reads=["dtx"], writes=["dtx"])
                  S.op("act", lambda e: e.activation(out=dtv[:], in_=dtx[:], func=AF.Ln, bias=1.0),
                       reads=["dtx"], writes=["dtv"])
                  S.op("act", lambda e: e.activation(out=aexp[:], in_=alog, func=AF.Exp), reads=["rowp"], writes=["aexp"])
                  S.op("dve", lambda e: e.scalar_tensor_tensor(
                      out=av[:].rearrange("p (c j) -> p c j", c=16), in0=dtv[:].rearrange("p (c j) -> p c j", c=16),
                      scalar=-1.0, in1=aexp[:].unsqueeze(1).to_broadcast([128, 16, 32]), op0=ALU.mult, op1=ALU.mult),
                       reads=["dtv", "aexp"], writes=["av"])
                  for c in range(16):
                      S.op("pe", lambda e, c=c: e.matmul(PS(5, c * 32, c * 32 + 16), lhsT=Um, rhs=av[:, c * 32:c * 32 + 16],
                                                       start=True, stop=True), reads=["cf", "av"], writes=[pk(5)], signal=False)
                      S.op("pe", lambda e, c=c: e.matmul(PS(5, c * 32 + 16, c * 32 + 32), lhsT=Lm,
                                                       rhs=av[:, c * 32 + 16:c * 32 + 32], start=True, stop=True),
                           reads=["cf", "av"], writes=[pk(5)], signal=False)
                      S.op("pe", lambda e, c=c: e.matmul(PS(6, c * 32, c * 32 + 32), lhsT=ones[:], rhs=av[:, c * 32:c * 32 + 32],
                                                       start=True, stop=True), reads=["ones", "av"], writes=[pk(6)],
                           signal=(c == 15))
                  S.op("act", lambda e: e.copy(out=cum[:], in_=PS(5)), reads=[pk(5)], writes=["cum"])
                  S.op("act", lambda e: e.copy(out=totv[:], in_=PS(6)), reads=[pk(6)], writes=["totv"])
                  S.op("act", lambda e: e.activation(out=eac[:], in_=cum[:], func=AF.Exp), reads=["cum"], writes=["eac"])
                  S.op("act", lambda e: e.activation(out=etot[:], in_=totv[:], func=AF.Exp), reads=["totv"], writes=["etot"])
                  S.op("dve", lambda e: e.tensor_tensor(out=cum[:], in0=totv[:], in1=cum[:], op=ALU.subtract),
                       reads=["totv", "cum"], writes=["cum"])
                  S.op("act", lambda e: e.activation(out=cum[:], in_=cum[:], func=AF.Exp), reads=["cum"], writes=["cum"])
                  S.op("dve", lambda e: e.tensor_tensor(out=wdtv[:], in0=dtv[:], in1=cum[:], op=ALU.mult),
                       reads=["dtv", "cum"], writes=["wdtv"])

                  S.barrier()
                  if stop == 3:
                      S.barrier()
                      S.emit(blk)
                      return nc
              Hst = sbt(s1, "Hst", [128, 4, 512], F32)
              Hbf = sbt(s1, "Hbf", [128, 512], BF16)
              xw = [sbt(s1, "xw%d" % i, [128, 512], BF16) for i in range(2)]
              xdt = [sbt(s1, "xdt%d" % i, [128, 512], BF16) for i in range(2)]
              rhs_all = [sbt(s1, "rhsall0", [128, 8, 128], F32)] * 2
              eseg = [sbt(s1, "eseg%d" % i, [128, 1024], BF16) for i in range(2)]
              MT = [sbt(s1, "MT%d" % i, [128, 1024], BF16) for i in range(2)]
              CBm = sbt(s1, "CBm", [128, 16, 2, 128], BF16)
              yb = sbt(s1, "yb", [128, 16, 512], BF16)
              ytmp = sbt(s1, "ytmp", [128, 512], F32)
              yv = sbt(s1, "yv", [128, 512], F32)
              xsd = sbt(s1, "xsd", [128, 512], F32)
              gn = sbt(s1, "gn", [128, 512], BF16)
              gjunk = sbt(s1, "gjunk", [128, 512], BF16)
              ssg = sbt(s1, "ssg", [128, 1], F32)
              totlog = sbt(s1, "totlog", [128, 32], F32)
              rtl = sbt(s1, "rtl", [128, 8, 32], F32)
              coef = sbt(s1, "coef", [128, 8, 32], F32)
              rk = [sbt(s1, "rk0", [128, 2048], F32)] * 2

              def idx(c, d, g):
                  return c * 32 + d * 16 + g * 8

              def state_update(c, d, g, hkey, Hap, bank):
                  i0 = idx(c, d, g)
                  b = (c + d) % 2
                  S.op("pool", lambda e, b=b, c=c, g=g, i0=i0: e.tensor_tensor(
                      out=xw[b][:].rearrange("p (e q) -> p e q", e=8),
                      in0=xs_tok[:, c, g * 512:(g + 1) * 512].rearrange("p (e q) -> p e q", e=8),
                      in1=bc3(wdtv[:, i0:i0 + 8], 64), op=ALU.mult), reads=["xs_tok", "wdtv"], writes=["xw%d" % b])
                  S.op("pe", lambda e, b=b, c=c, g=g, bank=bank: e.matmul(
                      PS(bank), lhsT=B_tok[:, c, g * 128:(g + 1) * 128], rhs=xw[b][:], start=True, stop=True),
                       reads=["B_tok", "xw%d" % b], writes=[pk(bank)])
                  S.op("dve", lambda e, i0=i0: e.tensor_tensor(
                      out=Hap.rearrange("p (e q) -> p e q", e=8), in0=Hap.rearrange("p (e q) -> p e q", e=8),
                      in1=bc3(etot[:, i0:i0 + 8], 64), op=ALU.mult), reads=[hkey, "etot"], writes=[hkey])
                  S.op("dve", lambda e, bank=bank: e.tensor_tensor(out=Hap, in0=Hap, in1=PS(bank), op=ALU.add),
                       reads=[hkey, pk(bank)], writes=[hkey])

              S.op("pool", lambda e: e.memset(Hst[:], 0.0), writes=["H0", "H1", "H2", "H3"])
              for g in range(2):
                  for d in range(2):
                      order = range(16) if d == 0 else range(15, -1, -1)
                      for c in order:
                          state_update(c, d, g, "H%d" % (g * 2 + d), Hst[:, g * 2 + d, :], 6 + (c % 2))
              S.op("dve", lambda e: e.tensor_reduce(out=totlog[:], in_=totv[:].rearrange("p (c j) -> p j c", c=16),
                                                    axis=AX.X, op=ALU.add), reads=["totv"], writes=["totlog"])
              S.dma("sp", send_d.ap()[:, 0:2048], Hst[:].rearrange("p a q -> p (a q)"), "snd",
                    reads=["H0", "H1", "H2", "H3"], writes=["send_d"])
              S.dma("sp", send_d.ap()[:, 2048:2080], totlog[:], "snd", reads=["totlog"], writes=["send_d"])
              S.custom("pool", lambda e: e.collective_compute(
                  "AllGather", ALU.bypass, replica_groups=[list(range(8))],
                  ins=[send_d.ap().opt()], outs=[recv_d.ap().opt()]), "cc", 1, reads=["send_d"], writes=["recv_d"])

              def cb_pre(g):
                  for c in range(16):
                      bank = 4 + (c % 2)
                      S.op("pe", lambda e, c=c, g=g, bank=bank: e.matmul(
                          PS(bank, 0, 128), lhsT=BT[:, g, c * 128:(c + 1) * 128], rhs=CT[:, g, c * 128:(c + 1) * 128],
                          start=True, stop=True), reads=["BT", "CT"], writes=[pk(bank)])
                      S.op("dve", lambda e, c=c, bank=bank: e.tensor_tensor(out=CBm[:, c, 0, :], in0=PS(bank, 0, 128),
                                                                          in1=Um, op=ALU.mult),
                           reads=[pk(bank), "cf"], writes=["CBm"])
                      S.op("dve", lambda e, c=c, bank=bank: e.tensor_tensor(out=CBm[:, c, 1, :], in0=PS(bank, 0, 128),
                                                                          in1=Lm, op=ALU.mult),
                           reads=[pk(bank), "cf"], writes=["CBm"])

              cb_pre(0)

              S.dma("sp", rtl[:], recv_d.ap()[:, 2048:2080].rearrange("(r p) c -> p r c", p=128), "rcvt",
                    reads=["recv_d"], writes=["rtl"])
              for r in range(8):
                  for d in range(2):
                      S.op("dve", lambda e, r=r, d=d: e.tensor_scalar(
                          out=rtl[:, r, d * 16:(d + 1) * 16], in0=rtl[:, r, d * 16:(d + 1) * 16],
                          scalar1=cmask[:, d * 8 + r:d * 8 + r + 1], scalar2=None, op0=ALU.mult),
                           reads=["rtl", "rowp"], writes=["rtl"])
              S.op("pool", lambda e: e.memset(coef[:], 0.0), reads=[], writes=["coef"])
              for r in range(6, -1, -1):
                  S.op("dve", lambda e, r=r: e.tensor_tensor(out=coef[:, r, 0:16], in0=coef[:, r + 1, 0:16],
                                                           in1=rtl[:, r + 1, 0:16], op=ALU.add),
                       reads=["coef", "rtl"], writes=["coef"])
              for r in range(1, 8):
                  S.op("dve", lambda e, r=r: e.tensor_tensor(out=coef[:, r, 16:32], in0=coef[:, r - 1, 16:32],
                                                           in1=rtl[:, r - 1, 16:32], op=ALU.add),
                       reads=["coef", "rtl"], writes=["coef"])
              S.op("act", lambda e: e.activation(out=coef[:], in_=coef[:], func=AF.Exp), reads=["coef"], writes=["coef"])
              for r in range(8):
                  for d in range(2):
                      S.op("dve", lambda e, r=r, d=d: e.tensor_scalar(
                          out=coef[:, r, d * 16:(d + 1) * 16], in0=coef[:, r, d * 16:(d + 1) * 16],
                          scalar1=cmask[:, d * 8 + r:d * 8 + r + 1], scalar2=None, op0=ALU.mult),
                           reads=["coef", "rowp"], writes=["coef"])
              S.op("pool", lambda e: e.memset(Hst[:], 0.0), reads=[], writes=["H0", "H1", "H2", "H3"])
              for r in range(8):
                  b = 0
                  S.dma("sp", rk[b][:], recv_d.ap()[r * 128:(r + 1) * 128, 0:2048], "rcv%d" % b,
                        reads=["recv_d"], writes=["rk%d" % b])
                  for g in range(2):
                      for d in range(2):
                          a = g * 2 + d
                          S.op("dve", lambda e, b=b, a=a, r=r, d=d, g=g: e.tensor_tensor(
                              out=rk[b][:, a * 512:(a + 1) * 512].rearrange("p (e q) -> p e q", e=8),
                              in0=rk[b][:, a * 512:(a + 1) * 512].rearrange("p (e q) -> p e q", e=8),
                              in1=bc3(coef[:, r, d * 16 + g * 8:d * 16 + g * 8 + 8], 64), op=ALU.mult),
                               reads=["rk%d" % b, "coef"], writes=["rk%d" % b])
                  S.op("dve", lambda e, b=b: e.tensor_tensor(out=Hst[:].rearrange("p a q -> p (a q)"),
                                                           in0=Hst[:].rearrange("p a q -> p (a q)"), in1=rk[b][:],
                                                           op=ALU.add), reads=["H0", "H1", "H2", "H3", "rk%d" % b],
                       writes=["H0", "H1", "H2", "H3"])

              if stop == 4:
                  S.barrier()
                  S.emit(blk)
                  return nc
              for g in range(2):
                  if g == 1:
                      cb_pre(1)
                  for d in (1, 0):
                      hkey = "H%d" % (g * 2 + d)
                      Hap = Hst[:, g * 2 + d, :]
                      segl = SG if d == 0 else SL
                      msk = Um if d == 0 else Lm
                      order = range(16) if d == 0 else range(15, -1, -1)
                      first = True
                      for c in order:
                          if stop == 45 and not (g == 0 and d == 1 and c == 15):
                              continue
                          if stop == 46 and not (g == 0 and d == 1):
                              continue
                          if stop == 47 and not (g == 0):
                              continue
                          if stop == 48 and not (g == 0 and d == 0 and c == 0):
                              continue
                          if stop == 49 and not (g == 0 and d == 0 and c < 2):
                              continue
                          if stop == 50 and not (g == 0 and d == 0):
                              continue
                          i0 = idx(c, d, g)
                          b = c % 2
                          sb0, sb1 = (0, 1) if b == 0 else (6, 7)
                          S.op("pool", lambda e, b=b, i0=i0, msk=msk: e.tensor_tensor(
                              out=rhs_all[b][:], in0=msk.unsqueeze(1).to_broadcast([128, 8, 128]),
                              in1=bc3(av[:, i0:i0 + 8], 128), op=ALU.mult), reads=["cf", "av"], writes=["rhsall0"])
                          for e8 in range(8):
                              bank = sb0 if e8 < 4 else sb1
                              S.op("pe", lambda e, bank=bank, e8=e8, b=b, segl=segl: e.matmul(
                                  PS(bank, (e8 % 4) * 128, (e8 % 4 + 1) * 128), lhsT=segl, rhs=rhs_all[b][:, e8, :],
                                  start=True, stop=True), reads=["cf", "rhsall0"], writes=[pk(bank)],
                                   signal=(e8 % 4 == 3))
                          S.op("act", lambda e, b=b, sb0=sb0: e.activation(out=eseg[b][:, 0:512], in_=PS(sb0), func=AF.Exp),
                               reads=[pk(sb0)], writes=["eseg%d" % b])
                          S.op("act", lambda e, b=b, sb1=sb1: e.activation(out=eseg[b][:, 512:1024], in_=PS(sb1), func=AF.Exp),
                               reads=[pk(sb1)], writes=["eseg%d" % b])
                          S.op("dve", lambda e, b=b, c=c, d=d: e.tensor_tensor(
                              out=MT[b][:].rearrange("p (e l) -> p e l", e=8),
                              in0=eseg[b][:].rearrange("p (e l) -> p e l", e=8),
                              in1=CBm[:, c, d, :].unsqueeze(1).to_broadcast([128, 8, 128]), op=ALU.mult),
                               reads=["eseg%d" % b, "CBm"], writes=["MT%d" % b])
                          S.op("pool", lambda e, b=b, c=c, g=g, i0=i0: e.tensor_tensor(
                              out=xdt[b][:].rearrange("p (e q) -> p e q", e=8),
                              in0=xs_tok[:, c, g * 512:(g + 1) * 512].rearrange("p (e q) -> p e q", e=8),
                              in1=bc3(dtv[:, i0:i0 + 8], 64), op=ALU.mult), reads=["xs_tok", "dtv"], writes=["xdt%d" % b])
                          S.op("act", lambda e, Hap=Hap: e.copy(out=Hbf[:], in_=Hap), reads=[hkey], writes=["Hbf"])
                          S.op("pe", lambda e, c=c, g=g: e.matmul(PS(3), lhsT=CT[:, g, c * 128:(c + 1) * 128], rhs=Hbf[:],
                                                                start=True, stop=True), reads=["CT", "Hbf"], writes=[pk(3)])
                          for e8 in range(8):
                              S.op("pe", lambda e, e8=e8, b=b: e.matmul(
                                  PS(2, e8 * 64, (e8 + 1) * 64), lhsT=MT[b][:, e8 * 128:(e8 + 1) * 128],
                                  rhs=xdt[b][:, e8 * 64:(e8 + 1) * 64], start=True, stop=True),
                                   reads=["MT%d" % b, "xdt%d" % b], writes=[pk(2)], signal=(e8 == 7))
                          S.op("dve", lambda e, i0=i0: e.tensor_tensor(
                              out=ytmp[:].rearrange("p (e q) -> p e q", e=8), in0=PS(3).rearrange("p (e q) -> p e q", e=8),
                              in1=bc3(eac[:, i0:i0 + 8], 64), op=ALU.mult), reads=[pk(3), "eac"], writes=["ytmp"])
                          if d == 1:
                              S.op("dve", lambda e, c=c: e.tensor_tensor(out=yb[:, c, :], in0=PS(2), in1=ytmp[:], op=ALU.add),
                                   reads=[pk(2), "ytmp"], writes=["yb"])
                          else:
                              S.op("dve", lambda e: e.tensor_tensor(out=yv[:], in0=PS(2), in1=ytmp[:], op=ALU.add),
                                   reads=[pk(2), "ytmp"], writes=["yv"])
                              S.op("pool", lambda e, c=c: e.tensor_tensor(out=yv[:], in0=yv[:], in1=yb[:, c, :], op=ALU.add),
                                   reads=["yv", "yb"], writes=["yv"])
                              S.op("pool", lambda e, c=c, g=g: e.tensor_tensor(
                                  out=xsd[:].rearrange("p (e q) -> p e q", e=8),
                                  in0=xs_tok[:, c, g * 512:(g + 1) * 512].rearrange("p (e q) -> p e q", e=8),
                                  in1=bc3(dsk[:, g * 8:(g + 1) * 8], 64), op=ALU.mult),
                                   reads=["xs_tok", "rowp"], writes=["xsd"])
                              S.op("dve", lambda e: e.tensor_tensor(out=yv[:], in0=yv[:], in1=xsd[:], op=ALU.add),
                                   reads=["yv", "xsd"], writes=["yv"])
                              S.op("dve", lambda e, c=c, g=g: e.tensor_tensor(out=yv[:], in0=yv[:],
                                                                            in1=zs[:, c, g * 512:(g + 1) * 512], op=ALU.mult),
                                   reads=["yv", "zs"], writes=["yv"])
                              if CUT == 1:
                                  state_update(c, d, g, hkey, Hap, 4)
                                  continue
                              S.op("act", lambda e: e.activation(out=gjunk[:], in_=yv[:], func=AF.Square, accum_out=ssg[:]),
                                   reads=["yv"], writes=["gjunk", "ssg"])
                              S.op("act", lambda e: e.activation(out=ssg[:], in_=ssg[:], func=AF.Sqrt, bias=EPS,
                                                                 scale=1.0 / 512), reads=["ssg"], writes=["ssg"])
                              S.op("dve", lambda e: e.reciprocal(out=ssg[:], in_=ssg[:]), reads=["ssg"], writes=["ssg"])
                              S.op("dve", lambda e: e.tensor_scalar(out=gn[:], in0=yv[:], scalar1=ssg[:, 0:1], scalar2=None,
                                                                    op0=ALU.mult), reads=["yv", "ssg"], writes=["gn"])
                              if CUT == 2:
                                  state_update(c, d, g, hkey, Hap, 4)
                                  continue
                              for t in range(4):
                                  S.op("pe", lambda e, t=t: e.transpose(out=PSB(5, t * 128, (t + 1) * 128),
                                                                      in_=gn[:, t * 128:(t + 1) * 128], identity=ident[:]),
                                       reads=["gn", "ident"], writes=[pk(5)], signal=(t == 3))
                              S.op("dve", lambda e, c=c, g=g: e.tensor_tensor(
                                  out=mixS[:, g * 4:(g + 1) * 4, c * 128:(c + 1) * 128],
                                  in0=PSB(5, 0, 512).rearrange("p (k t) -> p k t", k=4),
                                  in1=bc3(ssmw[:, g * 4:(g + 1) * 4], 128), op=ALU.mult),
                                   reads=[pk(5), "colp"], writes=["mixT"])
                          state_update(c, d, g, hkey, Hap, 4)
              if debug:
                  D0 = 32768
                  for i, (t, kk) in enumerate([(dtv, "dtv"), (av, "av"), (eac, "eac"), (etot, "etot"), (wdtv, "wdtv"), (totv, "totv")]):
                      S.dma("sp", dbg_d[:, D0 + i * 512:D0 + (i + 1) * 512], t[:], "dbg", reads=[kk])
                  S.dma("sp", dbg_d[:, D0 + 3072:D0 + 5120], Hst[:].rearrange("p a q -> p (a q)"), "dbg", reads=["H0", "H1", "H2", "H3"])
                  S.dma("sp", dbg_d[:, D0 + 5120:D0 + 5376], coef[:].rearrange("p a q -> p (a q)"), "dbg", reads=["coef"])
                  S.dma("sp", dbg_d[:, D0 + 5376:D0 + 5632], rtl[:].rearrange("p a q -> p (a q)"), "dbg", reads=["rtl"])
                  S.dma("pool", dbg_d[:, D0 + 6144:D0 + 7168], xs_tok[:, 0, :], "dbg", reads=["xs_tok"])
                  S.dma("pool", dbg_d[:, D0 + 7168:D0 + 8192], zs[:, 0, :], "dbg", reads=["zs"])
                  S.dma("pool", dbg_d[:, D0 + 8192:D0 + 8448], B_tok[:, 0, :], "dbg", reads=["B_tok"])
                  S.dma("pool", dbg_d[:, D0 + 8448:D0 + 8576], BT[:, 0, 0:128], "dbg", reads=["BT"])
                  S.dma("pool", dbg_d[:, D0 + 8576:D0 + 8704], CT[:, 0, 0:128], "dbg", reads=["CT"])
                  S.dma("pool", dbg_d[:, D0 + 8704:D0 + 9216], yb[:, 0, :], "dbg", reads=["yb"])
                  S.dma("pool", dbg_d[:, D0 + 9216:D0 + 9344], CBm[:, 0, 0, :], "dbg", reads=["CBm"])
                  S.dma("pool", dbg_d[:, D0 + 9344:D0 + 9472], CBm[:, 0, 1, :], "dbg", reads=["CBm"])
              S.barrier()
              if stop in (5, 45, 46, 47, 48, 49, 50):
                  S.barrier()
                  S.emit(blk)
                  return nc

        mixN = sbt(top, "mixN", [128, 8, 2048], BF16)
        mixk = lambda k: (mixS[:, k, :] if k < 8 else mixN[:, k - 8, :])
        with ExitStack() as s2:
            hT2 = sbt(s2, "hTb", [128, 8, NTOK], BF16)
            with ExitStack() as s2n:
                phase_norm(s2n, hT2, "b")
                S.barrier()
            wqkv = [sbt(s2, "wqkv%d" % i, [128, 8, 384], BF16) for i in range(2)]
            tab = [sbt(s2, "tab%d" % i, [128, 3200], BF16) for i in range(2)]
            qT = [sbt(s2, "qT%d" % i, [128, 2048], BF16) for i in range(2)]
            kT = [sbt(s2, "kT%d" % i, [128, NTOK], BF16) for i in range(2)]
            vp = [sbt(s2, "vp%d" % i, [128, NSLOT, 2, 65], BF16) for i in range(2)]
            sq = sbt(s2, "sq", [128, 256], F32)
            ssq = sbt(s2, "ssq", [128, 4], F32)
            qkn = sbt(s2, "qkn", [128, 256], BF16)
            eeb = [sbt(s2, "ee%d" % i, [128, 640], BF16) for i in range(2)]
            pT = [sbt(s2, "pT%d" % i, [128, 640], BF16) for i in range(2)]
            rec = sbt(s2, "rec", [128, 2], F32)
            yna = sbt(s2, "yna", [128, 128], BF16)
            wqk = sbt(s2, "wqk", [128, 1], F32)
            S.op("dve", lambda e: e.tensor_tensor(out=wqk[:], in0=qkw[:, 0:1], in1=qkw[:, 1:2], op=ALU.mult),
                 reads=["colp"], writes=["wqk"])
            S.op("pool", lambda e: e.memset(vp[0][:], 1.0), writes=["vp0"])
            S.op("pool", lambda e: e.memset(vp[1][:], 1.0), writes=["vp1"])

            def load_pair(j):
                b = j % 2
                for i, c0 in enumerate((2592, 3616, 4640)):
                    S.dma("pool", wqkv[b][:, :, i * 128:(i + 1) * 128], w_in_v[:, :, c0 + j * 128:c0 + (j + 1) * 128],
                          "wq%d" % b, writes=["wqkv%d" % b])

            def load_tab(h):
                b = h % 2
                S.dma("pool", tab[b][:], natab_d[h, :, :], "tab%d" % b, writes=["tab%d" % b])
                S.op("act", lambda e, b=b: e.activation(out=tab[b][:], in_=tab[b][:], func=AF.Exp),
                     reads=["tab%d" % b], writes=["tab%d" % b])

            SETS = [0, 1] + [2] * 12 + [3, 4]
            load_pair(0)
            for j in range(NAJ):
                b = j % 2
                if j + 1 < 8:
                    load_pair(j + 1)
                for u in range(NSLOT):
                    own = OWN0 <= u < OWN0 + 16
                    bank = 2 + (u % 2)
                    for k in range(8):
                        S.op("pe", lambda e, bank=bank, u=u, k=k, b=b: e.matmul(
                            PS(bank, 0, 384), lhsT=hT2[:, k, u * 128:(u + 1) * 128], rhs=wqkv[b][:, k, :],
                            start=(k == 0), stop=(k == 7)), reads=["hT", "wqkv%d" % b], writes=[pk(bank)],
                             signal=(k == 7))
                    S.op("act", lambda e, bank=bank, u=u, b=b: e.copy(
                        out=vp[b][:, u, :, 0:64], in_=PS(bank, 256, 384).rearrange("p (h d) -> p h d", h=2)),
                         reads=[pk(bank)], writes=["vp%d" % b])
                    if NAP < 2: continue
                    S.op("act", lambda e, bank=bank: e.activation(out=sq[:], in_=PS(bank, 0, 256), func=AF.Square),
                         reads=[pk(bank)], writes=["sq"])
                    S.op("dve", lambda e: e.tensor_reduce(out=ssq[:], in_=sq[:].rearrange("p (h d) -> p h d", h=4),
                                                          axis=AX.X, op=ALU.add), reads=["sq"], writes=["ssq"])
                    S.op("act", lambda e: e.activation(out=ssq[:], in_=ssq[:], func=AF.Sqrt, bias=EPS, scale=1.0 / 64),
                         reads=["ssq"], writes=["ssq"])
                    S.op("dve", lambda e: e.reciprocal(out=ssq[:], in_=ssq[:]), reads=["ssq"], writes=["ssq"])
                    S.op("dve", lambda e, bank=bank: e.tensor_tensor(
                        out=qkn[:].rearrange("p (h d) -> p h d", h=4),
                        in0=PS(bank, 0, 256).rearrange("p (h d) -> p h d", h=4), in1=bc3(ssq[:, 0:4], 64), op=ALU.mult),
                         reads=[pk(bank), "ssq"], writes=["qkn"])
                    if NAP < 3: continue
                    if NAX == 1: continue
                    tb = 4 + (u % 2)
                    if own:
                        S.op("pe", lambda e, tb=tb: e.transpose(out=PSB(tb, 0, 128), in_=qkn[:, 0:128], identity=ident[:]),
                             reads=["qkn", "ident"], writes=[pk(tb)], signal=False)
                    S.op("pe", lambda e, tb=tb: e.transpose(out=PSB(tb, 128, 256), in_=qkn[:, 128:256], identity=ident[:]),
                         reads=["qkn", "ident"], writes=[pk(tb)])
                    if own:
                        S.op("act", lambda e, tb=tb, u=u, b=b: e.copy(
                            out=qT[b][:, (u - OWN0) * 128:(u - OWN0 + 1) * 128], in_=PSB(tb, 0, 128)),
                             reads=[pk(tb)], writes=["qT%d" % b])
                    if True:
                        S.op("act", lambda e, tb=tb, u=u, b=b: e.copy(
                            out=kT[b][:, u * 128:(u + 1) * 128], in_=PSB(tb, 128, 256)),
                             reads=[pk(tb)], writes=["kT%d" % b])
                        continue
                    S.op("dve", lambda e, tb=tb, u=u, b=b: e.tensor_tensor(
                        out=kT[b][:, u * 128:(u + 1) * 128], in0=PSB(tb, 128, 256),
                        in1=wqk[:, 0:1].to_broadcast([128, 128]), op=ALU.mult), reads=[pk(tb), "wqk"],
                         writes=["kT%d" % b])
                S.op("pool", lambda e, b=b: e.tensor_tensor(out=kT[b][:], in0=kT[b][:],
                                                          in1=wqk[:, 0:1].to_broadcast([128, NTOK]), op=ALU.mult),
                     reads=["kT%d" % b, "wqk"], writes=["kT%d" % b])
                for h2 in range(2):
                    if NAP < 4: continue
                    load_tab(2 * j + h2)
                if NAQ:
                    continue
                for m in range(NAM):
                    for h2 in range(2):
                        h = 2 * j + h2
                        tbuf = h % 2
                        x = (m * 2 + h2) % 2
                        sbk = (0, 1) if x == 0 else (6, 7)
                        for r in range(5):
                            u = m + r
                            bank = sbk[0] if r < 4 else sbk[1]
                            S.op("pe", lambda e, bank=bank, r=r, u=u, h2=h2, b=b, m=m: e.matmul(
                                PS(bank, (r % 4) * 128, (r % 4 + 1) * 128),
                                lhsT=kT[b][64 * h2:64 * h2 + 64, u * 128:(u + 1) * 128],
                                rhs=qT[b][64 * h2:64 * h2 + 64, m * 128:(m + 1) * 128], start=True, stop=True),
                                 reads=["kT%d" % b, "qT%d" % b], writes=[pk(bank)], signal=(r >= 3))
                        S.op("act", lambda e, x=x, sbk=sbk: e.activation(out=eeb[x][:, 0:512], in_=PS(sbk[0]), func=AF.Exp,
                                                                       scale=0.125),
                             reads=[pk(sbk[0])], writes=["ee%d" % x])
                        S.op("act", lambda e, x=x, sbk=sbk: e.activation(out=eeb[x][:, 512:640], in_=PS(sbk[1], 0, 128),
                                                                       func=AF.Exp, scale=0.125),
                             reads=[pk(sbk[1])], writes=["ee%d" % x])
                        st_ = SETS[m]
                        S.op("pool", lambda e, x=x, tbuf=tbuf, st_=st_: e.tensor_tensor(
                            out=pT[x][:], in0=eeb[x][:], in1=tab[tbuf][:, st_ * 640:(st_ + 1) * 640], op=ALU.mult),
                             reads=["ee%d" % x, "tab%d" % tbuf], writes=["pT%d" % x])
                        for r in range(5):
                            u = m + r
                            S.op("pe", lambda e, r=r, u=u, h2=h2, x=x, b=b: e.matmul(
                                PS(3, h2 * 65, h2 * 65 + 65), lhsT=pT[x][:, r * 128:(r + 1) * 128],
                                rhs=vp[b][:, u, h2, :], start=(r == 0), stop=(r == 4)),
                                 reads=["pT%d" % x, "vp%d" % b], writes=[pk(3)], signal=(r == 4))
                    S.op("dve", lambda e: e.reciprocal(
                        out=rec[:], in_=PS(3, 0, 130).rearrange("p (h d) -> p h d", h=2)[:, :, 64]),
                         reads=[pk(3)], writes=["rec"])
                    S.op("dve", lambda e: e.tensor_tensor(
                        out=yna[:].rearrange("p (h d) -> p h d", h=2),
                        in0=PS(3, 0, 130).rearrange("p (h d) -> p h d", h=2)[:, :, 0:64], in1=bc3(rec[:, 0:2], 64),
                        op=ALU.mult), reads=[pk(3), "rec"], writes=["yna"])
                    S.op("pe", lambda e: e.transpose(out=PSB(5, 0, 128), in_=yna[:], identity=ident[:]),
                         reads=["yna", "ident"], writes=[pk(5)])
                    S.op("act", lambda e, j=j, m=m: e.copy(out=mixN[:, j, m * 128:(m + 1) * 128], in_=PSB(5, 0, 128)),
                         reads=[pk(5)], writes=["mixT"])
            S.barrier()
            if stop == 6:
                S.barrier()
                S.emit(blk)
                return nc

        if debug:
            with ExitStack() as sd:
                dbuf = sbt(sd, "dbuf", [128, 2048], F32)
                for k in range(16):
                    S.op("dve", lambda e, k=k: e.tensor_copy(out=dbuf[:], in_=mixk(k)), reads=["mixT"], writes=["dbuf"])
                    S.dma("sp", dbg_d[:, k * 2048:(k + 1) * 2048], dbuf[:], "dbg", reads=["dbuf"])
                S.barrier()

        with ExitStack() as s3:
            wo = [sbt(s3, "wo%d" % i, [128, 4, 512], BF16) for i in range(2)]
            w1b = [sbt(s3, "w1b%d" % i, [128, 8, 512], BF16) for i in range(2)]
            w2b = [sbt(s3, "w2b%d" % i, [128, 4, 512], BF16) for i in range(2)]
            x2 = sbt(s3, "x2", [128, 4, 1024], F32)
            h2n = sbt(s3, "h2n", [128, 1024], BF16)
            junk3 = sbt(s3, "junk3", [128, 1024], BF16)
            ss3 = sbt(s3, "ss3", [128, 1], F32)
            h2T = sbt(s3, "h2T", [128, 8, 512], BF16)
            uT = sbt(s3, "uT", [128, 32, 512], BF16)
            rl = [sbt(s3, "rl%d" % i, [128, 512], F32) for i in range(2)]
            ost = [sbt(s3, "ost%d" % i, [128, 512], F32) for i in range(2)]
            w_out_v = w_out_d.rearrange("(k p) n -> p k n", p=128)
            w1_v = w1_d.rearrange("(k p) f -> p k f", p=128)
            w2_v = w2_d.rearrange("(f p) n -> p f n", p=128)
            nst = 0
            for tg in range(4):
                for t in range(4):
                    tile_i = tg * 4 + t
                    S.dma("sp", x2[:, t, :], xs_d[(tile_i + OWN0) * 128:(tile_i + OWN0 + 1) * 128, :],
                          "xr%d" % (t % 2), writes=["x2_%d" % t])
                for half in range(2):
                    for kb in range(4):
                        wb = (half * 4 + kb) % 2
                        S.dma("pool", wo[wb][:], w_out_v[:, kb * 4:(kb + 1) * 4, half * 512:(half + 1) * 512],
                              "wo%d" % wb, writes=["wo%d" % wb])
                        for k4 in range(4):
                            k = kb * 4 + k4
                            for t in range(4):
                                tok = (tg * 4 + t) * 128
                                S.op("pe", lambda e, t=t, k=k, k4=k4, wb=wb, tok=tok: e.matmul(
                                    PS(t), lhsT=mixk(k)[:, tok:tok + 128], rhs=wo[wb][:, k4, :], start=(k == 0),
                                    stop=(k == 15)), reads=["mixT", "wo%d" % wb], writes=[pk(t)],
                                     signal=(k4 == 3 and t == 3))
                    for t in range(4):
                        S.op("dve", lambda e, t=t, half=half: e.tensor_tensor(
                            out=x2[:, t, half * 512:(half + 1) * 512], in0=PS(t),
                            in1=x2[:, t, half * 512:(half + 1) * 512], op=ALU.add),
                             reads=[pk(t), "x2_%d" % t], writes=["x2_%d" % t])
                for t in range(4):
                    S.op("act", lambda e, t=t: e.activation(out=junk3[:], in_=x2[:, t, :], func=AF.Square, accum_out=ss3[:]),
                         reads=["x2_%d" % t], writes=["junk3", "ss3"])
                    S.op("act", lambda e: e.activation(out=ss3[:], in_=ss3[:], func=AF.Sqrt, bias=EPS, scale=1.0 / 1024),
                         reads=["ss3"], writes=["ss3"])
                    S.op("dve", lambda e: e.reciprocal(out=ss3[:], in_=ss3[:]), reads=["ss3"], writes=["ss3"])
                    S.op("dve", lambda e, t=t: e.tensor_scalar(out=h2n[:], in0=x2[:, t, :], scalar1=ss3[:, 0:1], scalar2=None,
                                                             op0=ALU.mult), reads=["x2_%d" % t, "ss3"], writes=["h2n"])
                    tb = 4 + (t % 2)
                    for k in range(8):
                        S.op("pe", lambda e, tb=tb, k=k: e.transpose(out=PSB(tb, k * 128, (k + 1) * 128),
                                                                   in_=h2n[:, k * 128:(k + 1) * 128], identity=ident[:]),
                             reads=["h2n", "ident"], writes=[pk(tb)], signal=(k == 7))
                    S.op("dve", lambda e, tb=tb, t=t: e.tensor_tensor(
                        out=h2T[:, :, t * 128:(t + 1) * 128], in0=PSB(tb).rearrange("p (k t) -> p k t", k=8),
                        in1=bc3(nw2, 128), op=ALU.mult), reads=[pk(tb), "colp"], writes=["h2T"])
                for fb in range(8):
                    wb = fb % 2
                    S.dma("pool", w1b[wb][:], w1_v[:, :, fb * 512:(fb + 1) * 512], "w1%d" % wb, writes=["w1b%d" % wb])
                    for f4 in range(4):
                        f = fb * 4 + f4
                        bank = 6 + (f % 2)
                        for k in range(8):
                            S.op("pe", lambda e, bank=bank, wb=wb, k=k, f4=f4: e.matmul(
                                PS(bank), lhsT=w1b[wb][:, k, f4 * 128:(f4 + 1) * 128], rhs=h2T[:, k, :], start=(k == 0),
                                stop=(k == 7)), reads=["w1b%d" % wb, "h2T"], writes=[pk(bank)], signal=(k == 7))
                        S.op("act", lambda e, bank=bank, f=f: e.activation(out=rl[f % 2][:], in_=PS(bank), func=AF.Relu),
                             reads=[pk(bank)], writes=["rl%d" % (f % 2)])
                        eng = "dve" if f % 2 == 0 else "pool"
                        S.op(eng, lambda e, f=f: e.tensor_tensor(out=uT[:, f, :], in0=rl[f % 2][:], in1=rl[f % 2][:],
                                                               op=ALU.mult), reads=["rl%d" % (f % 2)], writes=["uT"])
                for half in range(2):
                    for fb in range(8):
                        wb = (half * 8 + fb) % 2
                        S.dma("pool", w2b[wb][:], w2_v[:, fb * 4:(fb + 1) * 4, half * 512:(half + 1) * 512],
                              "w2%d" % wb, writes=["w2b%d" % wb])
                        for f4 in range(4):
                            f = fb * 4 + f4
                            for t in range(4):
                                S.op("pe", lambda e, t=t, f=f, f4=f4, wb=wb: e.matmul(
                                    PS(t), lhsT=uT[:, f, t * 128:(t + 1) * 128], rhs=w2b[wb][:, f4, :], start=(f == 0),
                                    stop=(f == 31)), reads=["uT", "w2b%d" % wb], writes=[pk(t)],
                                     signal=(f4 == 3 and t == 3))
                    for t in range(4):
                        tile_i = tg * 4 + t
                        ob = nst % 2
                        nst += 1
                        S.op("dve", lambda e, t=t, half=half, ob=ob: e.tensor_tensor(
                            out=ost[ob][:], in0=PS(t), in1=x2[:, t, half * 512:(half + 1) * 512], op=ALU.add),
                             reads=[pk(t), "x2_%d" % t], writes=["ost%d" % ob])
                        S.dma("sp", out_d[tile_i * 128:(tile_i + 1) * 128, half * 512:(half + 1) * 512], ost[ob][:],
                              "st%d" % ob, reads=["ost%d" % ob])
            S.barrier()
        S.emit(blk)
    return nc


def _na_table(rpb, q):
    SETM = [0, 1, 2, 14, 15]
    a2 = np.arange(2)[:, None, None, None]
    kc = np.arange(64)[None, :, None, None]
    i2 = np.arange(2)[None, None, :, None]
    j = np.arange(64)[None, None, None, :]
    c0 = np.clip(j - 8, 0, 48)
    tab = np.full((16, 128, 5, 5, 128), NEG, np.float32)
    for si, m in enumerate(SETM):
        for rel in range(5):
            u = m + rel
            T = 16 * q + u - 2
            valid = True
            if q == 0 and u == 0:
                T = 3
            elif q == 0 and u == 1:
                valid = False
            elif q == 3 and u == 18:
                valid = False
            elif q == 3 and u == 19:
                T = 60
            if not valid or T < 0 or T > 63:
                continue
            a = 2 * T + a2
            i = 32 * q + 2 * m + i2
            r0 = np.clip(i - 4, 0, 120)
            ok = (a >= r0) & (a < r0 + 8) & (kc >= c0) & (kc < c0 + 16)
            ri = np.clip(a - i + 7, 0, 14)
            ci = np.clip(kc - j + 15, 0, 30)
            ri, ci, ok = np.broadcast_arrays(ri, ci, ok)
            vals = rpb[:, ri, ci]
            vals = np.where(ok[None], vals, np.float32(NEG))
            tab[:, :, si, rel, :] = vals.reshape(16, 128, 128)
    return np.ascontiguousarray(tab.reshape(16, 128, 3200))


def _slot_rows(x_b, q):
    rows = np.zeros((NTOK, 1024), np.float32)
    for u in range(NSLOT):
        T = 16 * q + u - 2
        if q == 0 and u == 0:
            T = 3
        elif q == 0 and u == 1:
            T = -1
        elif q == 3 and u == 18:
            T = -1
        elif q == 3 and u == 19:
            T = 60
        if 0 <= T <= 63:
            rows[u * 128:(u + 1) * 128] = x_b[T * 128:(T + 1) * 128]
    return rows


_CACHE = {}


def _prep_inputs(x, norm_mix_w, w_in, conv_w, conv_b, dt_bias_fwd, dt_bias_bwd, a_log_fwd, a_log_bwd,
                 d_skip, ssm_norm_w, q_norm_w, k_norm_w, rel_pos_bias, w_out, norm_mlp_w, w_mlp_in, w_mlp_out):
    f = lambda a: np.ascontiguousarray(np.asarray(a, dtype=np.float32))
    x = f(x)
    col = lambda v: f(v).reshape(8, 128).T
    cw = f(conv_w)[0].reshape(5, 12, 128).transpose(2, 1, 0).reshape(128, 60)
    cb = f(conv_b)[0].reshape(12, 128).T
    qk = np.stack([np.tile(f(q_norm_w)[0], 2), np.tile(f(k_norm_w)[0], 2)], axis=1)
    colp = np.concatenate([col(norm_mix_w[0]), col(norm_mlp_w[0]), col(ssm_norm_w[0]), cw, cb, qk], axis=1)
    colp = np.ascontiguousarray(colp, dtype=np.float32)
    rowv = np.concatenate([f(dt_bias_fwd)[0], f(dt_bias_bwd)[0], f(a_log_fwd)[0], f(a_log_bwd)[0], f(d_skip)[0]])
    in_maps = []
    w_in0, w_out0, w10, w20 = f(w_in)[0], f(w_out)[0], f(w_mlp_in)[0], f(w_mlp_out)[0]
    rpb = f(rel_pos_bias)[0]
    tabs = [_na_table(rpb, q) for q in range(4)]
    for core in range(8):
        b, q = divmod(core, 4)
        cm = np.array([1.0 if (r // 4 == b and r < core) else 0.0 for r in range(8)]
                      + [1.0 if (r // 4 == b and r > core) else 0.0 for r in range(8)], np.float32)
        rowp = np.ascontiguousarray(np.broadcast_to(np.concatenate([rowv, cm])[None, :], (128, 96)), dtype=np.float32)
        in_maps.append({
            "xs": _slot_rows(x[b], q), "w_in": w_in0, "w_out": w_out0, "w1": w10, "w2": w20,
            "colp": colp, "rowp": rowp, "natab": tabs[q],
        })
    return in_maps


def kernel(**inputs):
    debug = bool(inputs.pop("_debug", False))
    in_maps = _prep_inputs(**inputs)
    key = "dbg" if debug else "main"
    if key not in _CACHE:
        _CACHE[key] = build_program(debug=debug)
    nc = _CACHE[key]
    res = run_bass_kernel_spmd(nc, in_maps, core_ids=list(range(8)))
    out = np.zeros((2, 8192, 1024), np.float32)
    for core in range(8):
        b, q = divmod(core, 4)
        out[b, q * 2048:(q + 1) * 2048] = res.results[core]["out"]
    if debug:
        return out, [res.results[c]["dbg"] for c in range(8)]
    return out
```
